# Optimizing a Trainium2 kernel written in Bass

```python
import math
import jax
import jax.numpy as jnp
from jax import lax
import numpy as np

D_MODEL = 1024
BATCH = 16
SEQ = 2048
DEPTH = 2

PLE_DIM = 256
Q_BLOCK = 128
EPS = 1e-6
MLA_HEADS = 8
MLA_Q_RANK = 256
MLA_KV_RANK = 128
MLA_NOPE = 64
MLA_ROPE = 32
MLA_V = 64
ROPE_THETA = 10000.0
DIFF_HEADS = 4
DIFF_QK = 64
DIFF_V = 128
WIN_HEADS = 8
WIN_KV_HEADS = 2
WIN_GROUP = WIN_HEADS // WIN_KV_HEADS
WIN_HEAD_DIM = 64
WINDOW = 128
N_ALIBI = DIFF_HEADS + WIN_HEADS
N_BRANCH = 3
BRANCH_WIDTH = 512
D_FF = 2816
CONV_WIDTH = 3
IN_SPLITS = (MLA_Q_RANK, MLA_KV_RANK, MLA_ROPE,
             2 * DIFF_HEADS * DIFF_QK, 2 * DIFF_HEADS * DIFF_QK, DIFF_HEADS * DIFF_V,
             WIN_HEADS * WIN_HEAD_DIM, WIN_KV_HEADS * WIN_HEAD_DIM, WIN_KV_HEADS * WIN_HEAD_DIM,
             N_BRANCH * D_MODEL)
IN_COLS = sum(IN_SPLITS)

kernel_name = 'hybrid_mla_diff_swa_convffn_ple_encoder'


def rms_norm(x, g):
    xf = x.astype(jnp.float32)
    y = xf * lax.rsqrt(jnp.mean(xf * xf, axis=-1, keepdims=True) + EPS)
    return (y * g.astype(jnp.float32)).astype(x.dtype)


def rope(x, positions):
    half = x.shape[-1] // 2
    freqs = ROPE_THETA ** (-jnp.arange(half, dtype=jnp.float32) / half)
    ang = positions.astype(jnp.float32)[:, :, None, None] * freqs
    cos, sin = jnp.cos(ang), jnp.sin(ang)
    xf = x.astype(jnp.float32)
    x1, x2 = xf[..., :half], xf[..., half:]
    return jnp.concatenate([x1 * cos - x2 * sin, x2 * cos + x1 * sin], axis=-1).astype(x.dtype)


def alibi_slopes():
    return 2.0 ** (-8.0 * jnp.arange(1, N_ALIBI + 1, dtype=jnp.float32) / N_ALIBI)


def to_blocks(t):
    b, s = t.shape[:2]
    return jnp.moveaxis(t.reshape((b, s // Q_BLOCK, Q_BLOCK) + t.shape[2:]), 1, 0)


def from_blocks(t):
    t = jnp.moveaxis(t, 0, 1)
    return t.reshape((t.shape[0], t.shape[1] * t.shape[2]) + t.shape[3:])


def mla_attention(q, k, v):
    scale = (MLA_NOPE + MLA_ROPE) ** -0.5

    def block(qb):
        s = jnp.einsum('bqhd,bkhd->bhqk', qb, k, preferred_element_type=jnp.float32) * scale
        p = jax.nn.softmax(s, axis=-1).astype(v.dtype)
        return jnp.einsum('bhqk,bkhd->bqhd', p, v)

    return from_blocks(lax.map(block, to_blocks(q)))


def diff_attention(q, k, v, positions, lam, slopes):
    scale = DIFF_QK ** -0.5
    pos_f = positions.astype(jnp.float32)

    def block(args):
        qb, pb = args
        s = jnp.einsum('bqmhd,bkmhd->bmhqk', qb, k, preferred_element_type=jnp.float32) * scale
        dist = jnp.abs(pb[:, :, None] - pos_f[:, None, :])
        s = s - slopes[None, None, :, None, None] * dist[:, None, None]
        p = jax.nn.softmax(s, axis=-1)
        a = p[:, 0] - lam * p[:, 1]
        return jnp.einsum('bhqk,bkhd->bqhd', a.astype(v.dtype), v)

    return from_blocks(lax.map(block, (to_blocks(q), to_blocks(pos_f))))


def window_attention(q, k, v, positions, sink, slopes):
    b, s_len = q.shape[:2]
    nb = s_len // Q_BLOCK
    span = Q_BLOCK + 2 * WINDOW
    scale = WIN_HEAD_DIM ** -0.5
    pad = ((0, 0), (WINDOW, WINDOW), (0, 0), (0, 0))
    kp, vp = jnp.pad(k, pad), jnp.pad(v, pad)
    pos_f = positions.astype(jnp.float32)
    posp = jnp.pad(pos_f, ((0, 0), (WINDOW, WINDOW)))
    q_local = jnp.arange(Q_BLOCK)
    k_local = jnp.arange(span)
    band = jnp.abs(k_local[None, :] - WINDOW - q_local[:, None]) <= WINDOW
    qg = q.reshape(b, s_len, WIN_KV_HEADS, WIN_GROUP, WIN_HEAD_DIM)
    slope_hg = slopes.reshape(WIN_KV_HEADS, WIN_GROUP)[None, :, :, None, None]
    sink_hg = sink.astype(jnp.float32).reshape(WIN_KV_HEADS, WIN_GROUP)[None, :, :, None, None]

    def block(args):
        n, qb, pb = args
        start = n * Q_BLOCK
        kb = lax.dynamic_slice_in_dim(kp, start, span, axis=1)
        vb = lax.dynamic_slice_in_dim(vp, start, span, axis=1)
        pk = lax.dynamic_slice_in_dim(posp, start, span, axis=1)
        key_idx = start - WINDOW + k_local
        valid = band & ((key_idx >= 0) & (key_idx < s_len))[None, :]
        s = jnp.einsum('bqhgd,bkhd->bhgqk', qb, kb, preferred_element_type=jnp.float32) * scale
        dist = jnp.abs(pb[:, :, None] - pk[:, None, :])
        s = s - slope_hg * dist[:, None, None]
        s = jnp.where(valid, s, -jnp.inf)
        m = jnp.maximum(jnp.max(s, axis=-1, keepdims=True), sink_hg)
        e = jnp.exp(s - m)
        p = e / (jnp.sum(e, axis=-1, keepdims=True) + jnp.exp(sink_hg - m))
        return jnp.einsum('bhgqk,bkhd->bqhgd', p.astype(v.dtype), vb)

    out = from_blocks(lax.map(block, (jnp.arange(nb), to_blocks(qg), to_blocks(pos_f))))
    return out.reshape(b, s_len, WIN_HEADS * WIN_HEAD_DIM)


def depthwise_conv(a, w, bias):
    out = lax.conv_general_dilated(
        a, w[:, None, :].astype(a.dtype), window_strides=(1,),
        padding=((CONV_WIDTH // 2, CONV_WIDTH // 2),),
        dimension_numbers=('NWC', 'WIO', 'NWC'), feature_group_count=a.shape[-1])
    return out + bias.astype(a.dtype)


def hybrid_layer(x, pe, positions, layer_idx, g_mix, w_in, g_q_lora, w_uq, g_kv_lora, w_ukv,
                 g_mla_q, g_mla_k, g_diff_q, g_diff_k, lam_q1, lam_k1, lam_q2, lam_k2, g_diff_out,
                 g_win_q, g_win_k, win_sink, w_branch, w_out, g_ffn, w_ffn_gate, w_ffn_up,
                 conv_w, conv_b, w_ffn_down, w_ple_proj, g_ple, g_ple_in, w_ple_gate):
    b, s, _ = x.shape
    slopes = alibi_slopes()
    h = rms_norm(x, g_mix)
    proj = h @ w_in
    offs = np.cumsum(IN_SPLITS)[:-1].tolist()
    c_q, c_kv, k_r, d_q, d_k, d_v, s_q, s_k, s_v, gate_logits = jnp.split(proj, offs, axis=-1)

    q = (rms_norm(c_q, g_q_lora) @ w_uq).reshape(b, s, MLA_HEADS, MLA_NOPE + MLA_ROPE)
    kv = (rms_norm(c_kv, g_kv_lora) @ w_ukv).reshape(b, s, MLA_HEADS, MLA_NOPE + MLA_V)
    k_nope, v_mla = kv[..., :MLA_NOPE], kv[..., MLA_NOPE:]
    k_rope = jnp.broadcast_to(k_r[:, :, None, :], (b, s, MLA_HEADS, MLA_ROPE))
    k = rms_norm(jnp.concatenate([k_nope, k_rope], axis=-1), g_mla_k)
    q = rms_norm(q, g_mla_q)
    q = jnp.concatenate([q[..., :MLA_NOPE], rope(q[..., MLA_NOPE:], positions)], axis=-1)
    k = jnp.concatenate([k[..., :MLA_NOPE], rope(k[..., MLA_NOPE:], positions)], axis=-1)
    o_mla = mla_attention(q, k, v_mla).reshape(b, s, BRANCH_WIDTH)

    dq = rms_norm(d_q.reshape(b, s, 2, DIFF_HEADS, DIFF_QK), g_diff_q)
    dk = rms_norm(d_k.reshape(b, s, 2, DIFF_HEADS, DIFF_QK), g_diff_k)
    dv = d_v.reshape(b, s, DIFF_HEADS, DIFF_V)
    lam_init = 0.8 - 0.6 * math.exp(-0.3 * layer_idx)
    lam = (jnp.exp(jnp.sum(lam_q1.astype(jnp.float32) * lam_k1.astype(jnp.float32)))
           - jnp.exp(jnp.sum(lam_q2.astype(jnp.float32) * lam_k2.astype(jnp.float32))) + lam_init)
    od = diff_attention(dq, dk, dv, positions, lam, slopes[WIN_HEADS:])
    o_diff = (rms_norm(od, g_diff_out) * (1.0 - lam_init)).reshape(b, s, BRANCH_WIDTH)

    wq = rms_norm(s_q.reshape(b, s, WIN_HEADS, WIN_HEAD_DIM), g_win_q)
    wk = rms_norm(s_k.reshape(b, s, WIN_KV_HEADS, WIN_HEAD_DIM), g_win_k)
    wv = s_v.reshape(b, s, WIN_KV_HEADS, WIN_HEAD_DIM)
    o_win = window_attention(wq, wk, wv, positions, win_sink, slopes[:WIN_HEADS])

    gates = jax.nn.sigmoid(gate_logits.reshape(b, s, N_BRANCH, D_MODEL))
    merged = (gates[:, :, 0] * (o_mla @ w_branch[0])
              + gates[:, :, 1] * (o_diff @ w_branch[1])
              + gates[:, :, 2] * (o_win @ w_branch[2]))
    x = x + merged @ w_out

    h = rms_norm(x, g_ffn)
    a = depthwise_conv(h @ w_ffn_gate, conv_w, conv_b)
    x = x + (jax.nn.gelu(a) * (h @ w_ffn_up)) @ w_ffn_down

    e = rms_norm(pe @ w_ple_proj, g_ple)
    g = jax.nn.sigmoid(rms_norm(x, g_ple_in) @ w_ple_gate)
    return x + g * e


def setup_inputs(seed: int = 0) -> dict:
    key = jax.random.key(seed)
    ks = iter(jax.random.split(key, 40))

    def dense(shape, fan_in):
        return jax.random.normal(next(ks), shape, jnp.float32) * fan_in ** -0.5

    def gain(n):
        return 1.0 + 0.05 * jax.random.normal(next(ks), (DEPTH, n), jnp.float32)

    def small(shape, scale):
        return scale * jax.random.normal(next(ks), shape, jnp.float32)

    x = jax.random.normal(next(ks), (BATCH, SEQ, D_MODEL), jnp.float32)
    p = jax.random.normal(next(ks), (DEPTH, BATCH, SEQ, PLE_DIM), jnp.float32)
    offset = jax.random.randint(next(ks), (BATCH, 1), 0, 1024, dtype=jnp.int32)
    positions = offset + jnp.arange(SEQ, dtype=jnp.int32)[None, :]
    return {
        'x': x,
        'p': p,
        'positions': positions,
        'g_mix': gain(D_MODEL),
        'w_in': dense((DEPTH, D_MODEL, IN_COLS), D_MODEL),
        'g_q_lora': gain(MLA_Q_RANK),
        'w_uq': dense((DEPTH, MLA_Q_RANK, MLA_HEADS * (MLA_NOPE + MLA_ROPE)), MLA_Q_RANK),
        'g_kv_lora': gain(MLA_KV_RANK),
        'w_ukv': dense((DEPTH, MLA_KV_RANK, MLA_HEADS * (MLA_NOPE + MLA_V)), MLA_KV_RANK),
        'g_mla_q': gain(MLA_NOPE + MLA_ROPE),
        'g_mla_k': gain(MLA_NOPE + MLA_ROPE),
        'g_diff_q': gain(DIFF_QK),
        'g_diff_k': gain(DIFF_QK),
        'lam_q1': small((DEPTH, DIFF_QK), 0.1),
        'lam_k1': small((DEPTH, DIFF_QK), 0.1),
        'lam_q2': small((DEPTH, DIFF_QK), 0.1),
        'lam_k2': small((DEPTH, DIFF_QK), 0.1),
        'g_diff_out': gain(DIFF_V),
        'g_win_q': gain(WIN_HEAD_DIM),
        'g_win_k': gain(WIN_HEAD_DIM),
        'win_sink': small((DEPTH, WIN_HEADS), 1.0),
        'w_branch': dense((DEPTH, N_BRANCH, BRANCH_WIDTH, D_MODEL), BRANCH_WIDTH),
        'w_out': dense((DEPTH, D_MODEL, D_MODEL), D_MODEL),
        'g_ffn': gain(D_MODEL),
        'w_ffn_gate': dense((DEPTH, D_MODEL, D_FF), D_MODEL),
        'w_ffn_up': dense((DEPTH, D_MODEL, D_FF), D_MODEL),
        'conv_w': dense((DEPTH, CONV_WIDTH, D_FF), CONV_WIDTH),
        'conv_b': small((DEPTH, D_FF), 0.02),
        'w_ffn_down': dense((DEPTH, D_FF, D_MODEL), D_FF),
        'w_ple_proj': dense((DEPTH, PLE_DIM, D_MODEL), PLE_DIM),
        'g_ple': gain(D_MODEL),
        'g_ple_in': gain(D_MODEL),
        'w_ple_gate': dense((DEPTH, D_MODEL, D_MODEL), D_MODEL),
    }


def reference(x, p, positions, g_mix, w_in, g_q_lora, w_uq, g_kv_lora, w_ukv, g_mla_q, g_mla_k,
              g_diff_q, g_diff_k, lam_q1, lam_k1, lam_q2, lam_k2, g_diff_out, g_win_q, g_win_k,
              win_sink, w_branch, w_out, g_ffn, w_ffn_gate, w_ffn_up, conv_w, conv_b, w_ffn_down,
              w_ple_proj, g_ple, g_ple_in, w_ple_gate):
    for i in range(DEPTH):
        x = hybrid_layer(
            x, p[i], positions, i, g_mix[i], w_in[i], g_q_lora[i], w_uq[i], g_kv_lora[i], w_ukv[i],
            g_mla_q[i], g_mla_k[i], g_diff_q[i], g_diff_k[i], lam_q1[i], lam_k1[i], lam_q2[i],
            lam_k2[i], g_diff_out[i], g_win_q[i], g_win_k[i], win_sink[i], w_branch[i], w_out[i],
            g_ffn[i], w_ffn_gate[i], w_ffn_up[i], conv_w[i], conv_b[i], w_ffn_down[i],
            w_ple_proj[i], g_ple[i], g_ple_in[i], w_ple_gate[i])
    return x
```

```python
import math
from contextlib import ExitStack

import numpy as np
import concourse.bass as bass
import concourse.mybir as mybir
from concourse.bass_utils import run_bass_kernel_spmd

F32 = mybir.dt.float32
BF16 = mybir.dt.bfloat16
I32 = mybir.dt.int32
AF = mybir.ActivationFunctionType
ALU = mybir.AluOpType

S = 2048
DM = 1024
DEPTH = 2
BLK = 512
NBLK = S // BLK
NT = S // 128
DFF = 2816
NFC = DFF // 128
EPS = 1e-6
N_CORES = 8
SEQ_PER_CORE = 2
BIG = 1.0e5
TWO_PI = 2.0 * math.pi

O_CQ, O_CKV, O_KR, O_DQ, O_DK, O_DV, O_SQ, O_SK, O_SV, O_G = 0, 256, 384, 416, 928, 1440, 1952, 2464, 2592, 2720
CH_CQ, CH_CKV, CH_KR, CH_DQ, CH_DK, CH_DV, CH_SQ, CH_SK, CH_SV, CH_G = 0, 2, 3, 4, 8, 12, 16, 20, 22, 23
N_WIN_CH = 47

V_GMIX, V_GFFN, V_GPLE, V_GPLEIN = 0, 8, 16, 24
V_GQL = 32
V_GKVL = 34
V_GMQ, V_GMQS, V_GMK, V_GMKS = 35, 36, 37, 38
V_GDQ, V_GDK, V_GDO, V_GWQ, V_GWK = 39, 40, 41, 42, 43
V_CW = 44
V_CB = 110
V_LAM = 132
V_SINK = 388
NV = 396


def _win_cols():
    cols = np.full((N_WIN_CH, 128), -1, dtype=np.int64)
    r = np.arange(128)
    cols[0] = O_CQ + r
    cols[1] = O_CQ + 128 + r
    cols[2] = O_CKV + r
    j = np.arange(32)
    cols[3, 64:96] = O_KR + j
    cols[3, 96:128] = O_KR + (j + 16) % 32
    d = np.arange(64)
    for h in range(4):
        cols[CH_DQ + h, :64] = O_DQ + h * 64 + d
        cols[CH_DQ + h, 64:] = O_DQ + 256 + h * 64 + d
        cols[CH_DK + h, :64] = O_DK + h * 64 + d
        cols[CH_DK + h, 64:] = O_DK + 256 + h * 64 + d
        cols[CH_DV + h] = O_DV + h * 128 + r
        cols[CH_SQ + h] = O_SQ + h * 128 + r
    for g in range(2):
        cols[CH_SK + g, :64] = O_SK + g * 64 + d
        cols[CH_SK + g, 64:] = O_SK + g * 64 + d
    cols[CH_SV] = O_SV + r
    for i in range(3):
        for jj in range(8):
            cols[CH_G + i * 8 + jj] = O_G + i * 1024 + jj * 128 + r
    return cols


def _chunked(w, cols=None):
    K = w.shape[0]
    kc = K // 128
    if cols is None:
        n = w.shape[1] // 128
        return np.ascontiguousarray(w.reshape(kc, 128, n, 128).transpose(2, 1, 0, 3))
    wz = np.concatenate([w, np.zeros((K, 1), w.dtype)], axis=1)
    g = wz[:, cols.reshape(-1)].reshape(kc, 128, cols.shape[0], 128)
    return np.ascontiguousarray(g.transpose(2, 1, 0, 3))


def prep_shared(inp):
    cols = _win_cols()
    out = {}
    f = lambda a: np.asarray(a, dtype=np.float32)
    out["WIN"] = np.stack([_chunked(f(inp["w_in"][l]), cols) for l in range(DEPTH)])
    wuq = f(inp["w_uq"]).reshape(DEPTH, 2, 128, 8, 96)
    j = np.arange(32)
    sw = 64 + (j + 16) % 32
    WUQ = np.concatenate([wuq, wuq[..., sw]], axis=-1)
    out["WUQ"] = np.ascontiguousarray(WUQ.transpose(0, 2, 1, 3, 4))
    wukv = f(inp["w_ukv"]).reshape(DEPTH, 128, 8, 128)
    out["WUKV"] = np.ascontiguousarray(
        np.concatenate([wukv[..., :64].reshape(DEPTH, 128, 512), wukv[..., 64:].reshape(DEPTH, 128, 512)], axis=-1))
    wb = f(inp["w_branch"]).reshape(DEPTH, 3, 4, 128, 1024)
    out["WB"] = np.ascontiguousarray(wb.transpose(0, 1, 3, 2, 4))
    out["WOUT"] = np.stack([_chunked(f(inp["w_out"][l])) for l in range(DEPTH)])
    out["WG"] = np.stack([_chunked(f(inp["w_ffn_gate"][l])) for l in range(DEPTH)])
    out["WU"] = np.stack([_chunked(f(inp["w_ffn_up"][l])) for l in range(DEPTH)])
    out["WD"] = np.stack([_chunked(f(inp["w_ffn_down"][l])) for l in range(DEPTH)])
    out["WPP"] = np.stack([_chunked(f(inp["w_ple_proj"][l])) for l in range(DEPTH)])
    out["WPG"] = np.stack([_chunked(f(inp["w_ple_gate"][l])) for l in range(DEPTH)])
    vec = np.zeros((DEPTH, 128, NV), np.float32)
    for l in range(DEPTH):
        v = vec[l]
        v[:, V_GMIX:V_GMIX + 8] = f(inp["g_mix"][l]).reshape(8, 128).T
        v[:, V_GFFN:V_GFFN + 8] = f(inp["g_ffn"][l]).reshape(8, 128).T
        v[:, V_GPLE:V_GPLE + 8] = f(inp["g_ple"][l]).reshape(8, 128).T
        v[:, V_GPLEIN:V_GPLEIN + 8] = f(inp["g_ple_in"][l]).reshape(8, 128).T
        v[:, V_GQL:V_GQL + 2] = f(inp["g_q_lora"][l]).reshape(2, 128).T
        v[:, V_GKVL] = f(inp["g_kv_lora"][l])
        gq = f(inp["g_mla_q"][l]); gk = f(inp["g_mla_k"][l])
        v[:96, V_GMQ] = gq; v[:96, V_GMK] = gk
        v[64:96, V_GMQS] = gq[sw]; v[64:96, V_GMKS] = gk[sw]
        v[:, V_GDQ] = np.tile(f(inp["g_diff_q"][l]), 2)
        v[:, V_GDK] = np.tile(f(inp["g_diff_k"][l]), 2)
        v[:, V_GDO] = f(inp["g_diff_out"][l])
        v[:, V_GWQ] = np.tile(f(inp["g_win_q"][l]), 2)
        v[:, V_GWK] = np.tile(f(inp["g_win_k"][l]), 2)
        cw = f(inp["conv_w"][l]).reshape(3, NFC, 128)
        v[:, V_CW:V_CW + 66] = cw.transpose(2, 1, 0).reshape(128, 66)
        v[:, V_CB:V_CB + 22] = f(inp["conv_b"][l]).reshape(NFC, 128).T
        lam = np.concatenate([f(inp["lam_q1"][l]), f(inp["lam_k1"][l]), f(inp["lam_q2"][l]), f(inp["lam_k2"][l])])
        v[:, V_LAM:V_LAM + 256] = lam[None, :]
        v[:, V_SINK:V_SINK + 8] = f(inp["win_sink"][l])[None, :]
    out["VEC"] = vec
    cst = np.zeros((128, 4), np.float32)
    half = 16
    freqs = (10000.0 ** (-np.arange(half, dtype=np.float32) / half)).astype(np.float32)
    cst[64:96, 0] = np.tile(freqs, 2)
    cst[64:80, 1] = -1.0
    cst[80:96, 1] = 1.0
    out["CST"] = cst
    nm = np.zeros((6, 128, BLK), np.float32)
    pp = np.arange(128)[:, None]
    xx = np.arange(BLK)[None, :]
    for ri, r in enumerate(range(-1, 5)):
        nm[ri] = np.where(np.abs(128 * r + pp - xx) > 128, -BIG, 0.0)
    out["NM"] = np.ascontiguousarray(nm.transpose(1, 0, 2))
    out["IDENT"] = np.eye(128, dtype=np.float32)
    return out


def prep_core(inp, core):
    b0 = core * SEQ_PER_CORE
    x = np.asarray(inp["x"], np.float32)[b0:b0 + SEQ_PER_CORE]
    XT = np.ascontiguousarray(x.reshape(SEQ_PER_CORE, S, 8, 128).transpose(0, 3, 2, 1))
    p = np.asarray(inp["p"], np.float32)[:, b0:b0 + SEQ_PER_CORE]
    PT = np.ascontiguousarray(p.reshape(DEPTH, SEQ_PER_CORE, S, 2, 128).transpose(0, 1, 4, 3, 2))
    pos = np.asarray(inp["positions"], np.int32)[b0:b0 + SEQ_PER_CORE]
    POSB = np.ascontiguousarray(pos.reshape(SEQ_PER_CORE, 1, S))
    POSK = np.ascontiguousarray(pos.reshape(SEQ_PER_CORE, NT, 128).transpose(0, 2, 1))
    return {"XT": XT, "PT": PT, "POSB": POSB, "POSK": POSK}


class Sched:
    ENGS = ("pe", "act", "dve", "pool", "sp")

    def __init__(self, nc, es, same_engine_sync=True):
        self.nc = nc
        self.sem = {k: es.enter_context(nc.semaphore("sem_" + k)) for k in self.ENGS}
        self.cnt = {k: 0 for k in self.ENGS}
        self.seen = {k: {} for k in self.ENGS}
        self.prog = {k: [] for k in self.ENGS}
        self.lw = {}
        self.rd = {}
        self.dsem = {}
        self.es = es
        self.same = same_engine_sync
        self.n_ins = 0

    def _deps(self, eng, reads, writes):
        need = {}

        def add(tok):
            if tok is None:
                return
            k, v = tok
            if k == eng and (eng == "pe" or not self.same):
                return
            if need.get(k, 0) < v:
                need[k] = v

        for h in reads:
            add(self.lw.get(h))
        for h in writes:
            add(self.lw.get(h))
            for t in self.rd.get(h, ()):
                add(t)
        waits = []
        for k, v in need.items():
            if self.seen[eng].get(k, 0) < v:
                self.seen[eng][k] = v
                waits.append((k, v))
        return waits

    def _commit(self, tok, reads, writes):
        for h in reads:
            self.rd.setdefault(h, []).append(tok)
        for h in writes:
            self.lw[h] = tok
            self.rd[h] = []

    def op(self, eng, fn, reads=(), writes=()):
        waits = self._deps(eng, reads, writes)
        self.cnt[eng] += 1
        tok = (eng, self.cnt[eng])
        self.prog[eng].append((waits, fn, self.sem[eng], 1))
        self._commit(tok, reads, writes)
        return tok

    def dma(self, queue, key, fn, ndma, reads=(), writes=()):
        if key not in self.dsem:
            self.dsem[key] = [self.es.enter_context(self.nc.semaphore("dsem_%d" % len(self.dsem))), 0]
        waits = self._deps(queue, reads, writes)
        self.dsem[key][1] += 16 * ndma
        tok = (("d", key), self.dsem[key][1])
        self.prog[queue].append((waits, fn, self.dsem[key][0], None))
        self._commit(tok, reads, writes)
        return tok

    def _semh(self, k):
        if isinstance(k, str):
            return self.sem[k]
        return self.dsem[k[1]][0]

    def barrier(self):
        toks = [(k, self.cnt[k]) for k in self.ENGS if self.cnt[k] > 0]
        toks += [(("d", key), v[1]) for key, v in self.dsem.items() if v[1] > 0]
        for e in self.ENGS:
            waits = []
            for k, v in toks:
                if k == e and e == "pe":
                    continue
                if self.seen[e].get(k, 0) < v:
                    self.seen[e][k] = v
                    waits.append((k, v))
            if waits:
                self.prog[e].append((waits, None, None, None))

    def flush(self):
        prog = self.prog
        self.prog = {k: [] for k in self.ENGS}
        if not any(prog.values()):
            return

        def mk(ename):
            def body(e):
                for waits, fn, sem, inc in prog[ename]:
                    for k, v in waits:
                        e.wait_ge(self._semh(k), v)
                        self.n_ins += 1
                    if fn is None:
                        continue
                    self.n_ins += 1
                    if inc is None:
                        fn(e, sem)
                    else:
                        ins = fn(e)
                        ins.then_inc(sem, inc)
            return body

        with self.nc.Block() as block:
            block.tensor(mk("pe"))
            block.scalar(mk("act"))
            block.vector(mk("dve"))
            block.gpsimd(mk("pool"))
            block.sync(mk("sp"))


def build_program(n_seq=SEQ_PER_CORE, n_layers=DEPTH, stop_after=None, dbg=False, same_sync=True):
    nc = bass.Bass("TRN2", target_bir_lowering=False)
    dt = nc.dram_tensor
    XT = dt("XT", [SEQ_PER_CORE, 128, 8, S], F32, kind="ExternalInput").ap()
    PT = dt("PT", [DEPTH, SEQ_PER_CORE, 128, 2, S], F32, kind="ExternalInput").ap()
    POSB = dt("POSB", [SEQ_PER_CORE, 1, S], I32, kind="ExternalInput").ap()
    POSK = dt("POSK", [SEQ_PER_CORE, 128, NT], I32, kind="ExternalInput").ap()
    WIN = dt("WIN", [DEPTH, N_WIN_CH, 128, 8, 128], F32, kind="ExternalInput").ap()
    WUQ = dt("WUQ", [DEPTH, 128, 2, 8, 128], F32, kind="ExternalInput").ap()
    WUKV = dt("WUKV", [DEPTH, 128, 1024], F32, kind="ExternalInput").ap()
    WB = dt("WB", [DEPTH, 3, 128, 4, 1024], F32, kind="ExternalInput").ap()
    WOUT = dt("WOUT", [DEPTH, 8, 128, 8, 128], F32, kind="ExternalInput").ap()
    WG = dt("WG", [DEPTH, NFC, 128, 8, 128], F32, kind="ExternalInput").ap()
    WU = dt("WU", [DEPTH, NFC, 128, 8, 128], F32, kind="ExternalInput").ap()
    WD = dt("WD", [DEPTH, 8, 128, NFC, 128], F32, kind="ExternalInput").ap()
    WPP = dt("WPP", [DEPTH, 8, 128, 2, 128], F32, kind="ExternalInput").ap()
    WPG = dt("WPG", [DEPTH, 8, 128, 8, 128], F32, kind="ExternalInput").ap()
    VEC = dt("VEC", [DEPTH, 128, NV], F32, kind="ExternalInput").ap()
    CST = dt("CST", [128, 4], F32, kind="ExternalInput").ap()
    NM = dt("NM", [128, 6, BLK], F32, kind="ExternalInput").ap()
    IDENT = dt("IDENT", [128, 128], F32, kind="ExternalInput").ap()
    YT = dt("YT", [SEQ_PER_CORE, 128, 8, S], F32, kind="ExternalOutput").ap()
    dbg_names = []

    class Stop(Exception):
        pass

    es = ExitStack()
    with es:
        sc = Sched(nc, es, same_engine_sync=same_sync)

        uniq = [0]

        def sb(name, shape, dtype, st=es):
            uniq[0] += 1
            return st.enter_context(nc.sbuf_tensor("%s_u%d" % (name, uniq[0]), shape, dtype))

        xT = sb("xT", [128, 8, S], F32)
        hT = sb("hT", [128, 8, S], BF16)
        vec = sb("vec", [128, DEPTH, NV], F32)
        cst = sb("cst", [128, 4], F32)
        ones = sb("ones", [128, 128], BF16)
        bd = sb("bd", [128, 128], BF16)
        ident = sb("ident", [128, 128], BF16)
        lamt = sb("lamt", [128, DEPTH, 4], F32)
        esink = sb("esink", [128, DEPTH, 8], F32)
        epsc = sb("epsc", [128, 1], F32)
        xbias = sb("xbias", [128, DEPTH], F32)
        lscr = sb("lscr", [128, 64], F32)
        cfc = sb("cfc", [128, 12], F32)
        P = [es.enter_context(nc.psum_tensor("ps%d" % i, [128, BLK], F32)) for i in range(8)]
        PH = ["ps%d" % i for i in range(8)]

        class Rot:
            def __init__(self, name, n, shape, dtype, st):
                self.t = [sb("%s%d" % (name, i), shape, dtype, st) for i in range(n)]
                self.h = ["%s%d" % (name, i) for i in range(n)]
                self.i = 0

            def get(self):
                j = self.i % len(self.t)
                self.i += 1
                return self.t[j], self.h[j]

        class phase:
            def __enter__(self):
                self.st = ExitStack()
                self.st.__enter__()
                return self.st

            def __exit__(self, et, ev, tb):
                if et is None:
                    sc.barrier()
                    sc.flush()
                return self.st.__exit__(et, ev, tb)

        def mm(out, pairs, reads, writes, start=True, stop=True):
            def fn(e):
                ins = None
                n = len(pairs)
                for i, (l, r) in enumerate(pairs):
                    ins = e.matmul(out, l, r, start=(start and i == 0), stop=(stop and i == n - 1))
                return ins
            return sc.op("pe", fn, reads, writes)

        def act(out, in_, func, reads, writes, **kw):
            return sc.op("act", lambda e: e.activation(out=out, in_=in_, func=func, **kw), reads, writes)

        def stt(eng, out, in0, scalar, in1, op0, op1, reads, writes):
            return sc.op(eng, lambda e: e.scalar_tensor_tensor(out=out, in0=in0, scalar=scalar, in1=in1, op0=op0, op1=op1),
                         reads, writes)

        def ts(eng, out, in0, s1, s2, op0, op1, reads, writes):
            if op1 is None:
                return sc.op(eng, lambda e: e.tensor_scalar(out=out, in0=in0, scalar1=s1, scalar2=None, op0=op0), reads, writes)
            return sc.op(eng, lambda e: e.tensor_scalar(out=out, in0=in0, scalar1=s1, scalar2=s2, op0=op0, op1=op1), reads, writes)

        def tt(eng, out, in0, in1, op, reads, writes):
            return sc.op(eng, lambda e: e.tensor_tensor(out=out, in0=in0, in1=in1, op=op), reads, writes)

        def cp(eng, out, in_, reads, writes):
            return sc.op(eng, lambda e: e.tensor_copy(out=out, in_=in_), reads, writes)

        def ms(eng, ap, val, writes):
            return sc.op(eng, lambda e: e.memset(ap, val), (), writes)

        def load(queue, key, out, in_, writes, reads=()):
            return sc.dma(queue, key, lambda e, sem: e.dma_start(out=out, in_=in_).then_inc(sem, 16), 1, reads, writes)

        def dump(name, tile_ap, shape, dtype, reads):
            if not dbg:
                return
            d = dt("DBG_" + name, shape, dtype, kind="ExternalOutput").ap()
            dbg_names.append("DBG_" + name)
            load("sp", "dbg_" + name, d, tile_ap, ["dbgout_" + name], reads)

        def vcol(l, c, p0=0, p1=128):
            return vec[p0:p1, l, c:c + 1]

        def xh(b):
            return ["xT%d_%d" % (k, b) for k in range(8)]

        def hh_(b):
            return ["hT%d_%d" % (k, b) for k in range(8)]

        ms("dve", ones[:], 1.0, ["ones"])
        ms("dve", bd[:], 0.0, ["bd"])
        ms("dve", bd[0:64, 0:64], 1.0, ["bd"])
        ms("dve", bd[64:128, 64:128], 1.0, ["bd"])
        ms("dve", epsc[:], EPS, ["epsc"])
        for i_ in range(12):
            ms("dve", cfc[:, i_:i_ + 1], (2.0 ** (-8.0 * (i_ + 1) / 12.0)) / (64.0 ** -0.5), ["cfc"])
        for l in range(DEPTH):
            ms("dve", xbias[:, l:l + 1], math.log(1.0 - (0.8 - 0.6 * math.exp(-0.3 * l))), ["xbias"])
        load("sp", "cst", cst[:], CST[:, :], ["cst"])
        load("sp", "vec", vec[:], VEC.rearrange("l p n -> p l n"), ["vec"])
        load("pool", "ident", ident[:], IDENT[:, :], ["ident"])
        for l in range(DEPTH):
            for q in range(2):
                a = vec[:, l, V_LAM + q * 128: V_LAM + q * 128 + 64]
                b_ = vec[:, l, V_LAM + q * 128 + 64: V_LAM + q * 128 + 128]
                tt("dve", lscr[:], a, b_, ALU.mult, ["vec"], ["lscr"])
                sc.op("dve", lambda e, l=l: e.tensor_reduce(out=lamt[:, l, 3:4], in_=lscr[:], axis=mybir.AxisListType.X, op=ALU.add),
                      ["lscr"], ["lamt3"])
                act(lamt[:, l, 1 + q:2 + q], lamt[:, l, 3:4], AF.Exp, ["lamt3"], ["lamt%d" % (1 + q)])
            lam_init = 0.8 - 0.6 * math.exp(-0.3 * l)
            stt("dve", lamt[:, l, 0:1], lamt[:, l, 1:2], lam_init, lamt[:, l, 2:3], ALU.add, ALU.subtract,
                ["lamt1", "lamt2"], ["lam"])
            act(esink[:, l, :], vec[:, l, V_SINK:V_SINK + 8], AF.Exp, ["vec"], ["esink"])
        sc.barrier()
        sc.flush()

        def make_norm_tools(st, n=2):
            tools = {}
            tools["sq"] = Rot("sq", n + 1, [128, BLK], BF16, st)
            tools["ln"] = Rot("ln", n, [128, BLK], F32, st)
            tools["rs"] = Rot("rs", n, [128, BLK], F32, st)
            return tools

        def rstd(tools, ps_ap, psh, nfeat, lo, hi, layer_bias=None):
            ln, lh = tools["ln"].get()
            rs, rh = tools["rs"].get()
            sc.op("act", lambda e: e.activation(out=ln[lo:hi, :], in_=ps_ap, func=AF.Ln, scale=1.0 / nfeat, bias=epsc[lo:hi, 0:1]),
                  [psh, "epsc"], [lh])
            if layer_bias is None:
                sc.op("act", lambda e: e.activation(out=rs[lo:hi, :], in_=ln[lo:hi, :], func=AF.Exp, scale=-0.5), [lh], [rh])
            else:
                sc.op("act", lambda e: e.activation(out=rs[lo:hi, :], in_=ln[lo:hi, :], func=AF.Exp, scale=-0.5,
                                                    bias=xbias[lo:hi, layer_bias:layer_bias + 1]), [lh, "xbias"], [rh])
            return rs, rh

        def norm_block(tools, l, gcol, b, dst, dst_cs, dst_h):
            cs = slice(b * BLK, (b + 1) * BLK)
            for k in range(8):
                t, h = tools["sq"].get()
                act(t[:], xT[:, k, cs], AF.Square, ["xT%d_%d" % (k, b)], [h])
                mm(P[0][:], [(ones[:], t[:])], [h, "ones"], [PH[0]], start=(k == 0), stop=(k == 7))
            rs, rh = rstd(tools, P[0][:], PH[0], DM, 0, 128)
            for k in range(8):
                stt("dve", dst[:, k, dst_cs], xT[:, k, cs], vcol(l, gcol + k), rs[:], ALU.mult, ALU.mult,
                    ["xT%d_%d" % (k, b), "vec", rh], [dst_h(k, b)])

        def merge(l, i, nheads, oT):
            with phase() as st:
                wb = sb("wb", [128, 4, DM], BF16, st)
                load("pool", "wb", wb[:], WB[l, i], ["wb"])
                mg = sb("mg", [128, 8, 2 * BLK], BF16, st)
                wrot = Rot("wch", 4, [128, 8, 128], BF16, st)
                sg = Rot("sg", 3, [128, BLK], F32, st)
                for sbk in range(2):
                    for j in range(8):
                        w, wh = wrot.get()
                        load("pool", wh, w[:], WIN[l, CH_G + i * 8 + j], [wh])
                        for sub in range(2):
                            b = sbk * 2 + sub
                            cs = slice(b * BLK, (b + 1) * BLK)
                            mm(P[sub][:], [(w[:, k, :], hT[:, k, cs]) for k in range(8)], hh_(b) + [wh], [PH[sub]])
                            mm(P[2 + sub][:], [(wb[:, kc, j * 128:(j + 1) * 128], oT[:, kc, cs]) for kc in range(4)],
                               ["oT_%d_%d" % (h, b) for h in range(nheads)] + ["wb"], [PH[2 + sub]])
                            s_, sh = sg.get()
                            act(s_[:], P[sub][:], AF.Tanh, [PH[sub]], [sh], scale=0.5)
                            stt("dve", mg[:, j, sub * BLK:(sub + 1) * BLK], s_[:], 1.0, P[2 + sub][:], ALU.add, ALU.mult,
                                [sh, PH[2 + sub]], ["mg%d_%d" % (j, sub)])
                    for j in range(8):
                        w, wh = wrot.get()
                        load("pool", wh, w[:], WOUT[l, j], [wh])
                        for sub in range(2):
                            b = sbk * 2 + sub
                            cs = slice(b * BLK, (b + 1) * BLK)
                            mm(P[4 + sub][:], [(w[:, k, :], mg[:, k, sub * BLK:(sub + 1) * BLK]) for k in range(8)],
                               ["mg%d_%d" % (k, sub) for k in range(8)] + [wh], [PH[4 + sub]])
                            stt("dve", xT[:, j, cs], P[4 + sub][:], 0.5, xT[:, j, cs], ALU.mult, ALU.add,
                                [PH[4 + sub], "xT%d_%d" % (j, b)], ["xT%d_%d" % (j, b)])

        def check_stop(tag):
            if stop_after == tag:
                raise Stop()

        stopped = False
        if True:
            for s in range(n_seq):
                for k in range(8):
                    load("sp", "x%d" % k, xT[:, k, :], XT[s, :, k, :], ["xT%d_%d" % (k, b) for b in range(NBLK)])
                with ExitStack() as seq_es:
                    posq = sb("posq", [128, S], F32, seq_es)
                    npk = sb("npk", [128, NT], F32, seq_es)
                    nkc = sb("nkc", [128, NT, 12], F32, seq_es)
                    cosT = sb("cosT", [128, S], BF16, seq_es)
                    sinT = sb("sinT", [128, S], BF16, seq_es)
                    with phase() as t_es:
                        posqi = sb("posqi", [128, S], I32, t_es)
                        poski = sb("poski", [128, NT], I32, t_es)
                        load("sp", "posqi", posqi[:], POSB[s, 0, :].partition_broadcast(128), ["posqi"])
                        load("sp", "poski", poski[:], POSK[s, :, :], ["poski"])
                        cp("dve", posq[:], posqi[:], ["posqi"], ["posq"])
                        cp("dve", npk[:], poski[:], ["poski"], ["npk"])
                        ts("dve", npk[:], npk[:], -1.0, None, ALU.mult, None, ["npk"], ["npk"])
                        for j_ in range(12):
                            ts("dve", nkc[:, :, j_], npk[:], (2.0 ** (-8.0 * (j_ + 1) / 12.0)) / (64.0 ** -0.5), None, ALU.mult, None,
                               ["npk"], ["nkc"])
                        ang = sb("ang", [128, BLK], F32, t_es)
                        nn = sb("nn", [128, BLK], F32, t_es)
                        ni = sb("ni", [128, BLK], I32, t_es)
                        rr = sb("rr", [128, BLK], F32, t_es)
                        mk = sb("mk", [128, BLK], F32, t_es)
                        R = slice(64, 96)
                        for b in range(NBLK):
                            cs = slice(b * BLK, (b + 1) * BLK)
                            for which in range(2):
                                ts("dve", ang[R, :], posq[R, cs], cst[R, 0:1], (math.pi / 2 if which else 0.0), ALU.mult, ALU.add,
                                   ["posq", "cst"], ["ang"])
                                ts("dve", nn[R, :], ang[R, :], 1.0 / TWO_PI, None, ALU.mult, None, ["ang"], ["nn"])
                                cp("dve", ni[R, :], nn[R, :], ["nn"], ["ni"])
                                cp("dve", nn[R, :], ni[R, :], ["ni"], ["nn"])
                                stt("dve", rr[R, :], nn[R, :], -TWO_PI, ang[R, :], ALU.mult, ALU.add, ["nn", "ang"], ["rr"])
                                ts("dve", mk[R, :], rr[R, :], math.pi, None, ALU.is_gt, None, ["rr"], ["mk"])
                                stt("dve", rr[R, :], mk[R, :], -TWO_PI, rr[R, :], ALU.mult, ALU.add, ["mk", "rr"], ["rr"])
                                ts("dve", mk[R, :], rr[R, :], -math.pi, None, ALU.is_lt, None, ["rr"], ["mk"])
                                stt("dve", rr[R, :], mk[R, :], TWO_PI, rr[R, :], ALU.mult, ALU.add, ["mk", "rr"], ["rr"])
                                ts("dve", rr[R, :], rr[R, :], math.pi, -math.pi, ALU.min, ALU.max, ["rr"], ["rr"])
                                if which == 0:
                                    act(ang[R, :], rr[R, :], AF.Sin, ["rr"], ["ang"])
                                    ts("dve", sinT[R, cs], ang[R, :], cst[R, 1:2], None, ALU.mult, None, ["ang", "cst"], ["sinT"])
                                else:
                                    act(cosT[R, cs], rr[R, :], AF.Sin, ["rr"], ["cosT"])
                        dump("cosT", cosT[64:96, :], [32, S], BF16, ["cosT"])
                        dump("sinT", sinT[64:96, :], [32, S], BF16, ["sinT"])
                    stopped = (stop_after == "rope")

                    def run_layer(l):
                        stacks = []
                        try:
                            lam_init = 0.8 - 0.6 * math.exp(-0.3 * l)
                            with phase() as st:
                                tools = make_norm_tools(st)
                                for b in range(NBLK):
                                    norm_block(tools, l, V_GMIX, b, hT, slice(b * BLK, (b + 1) * BLK), lambda k, b: "hT%d_%d" % (k, b))
                                if l == 0 and s == 0:
                                    dump("hT", hT[:, :, :], [128, 8, S], BF16, sum([hh_(b) for b in range(NBLK)], []))
                            if stop_after == "norm":
                                return True
                            mix_es = ExitStack()
                            stacks.append(mix_es)
                            mix_es.__enter__()
                            oT = sb("oT", [128, 4, S], BF16, mix_es)
                            mla_es = ExitStack()
                            stacks.append(mla_es)
                            mla_es.__enter__()
                            cqn = sb("cqn", [128, 2, S], BF16, mla_es)
                            ckvn = sb("ckvn", [128, S], BF16, mla_es)
                            krr = sb("krr", [128, S], BF16, mla_es)
                            sqkr = sb("sqkr", [128, S], BF16, mla_es)
                            wuq = sb("wuq", [128, 2, 8, 128], BF16, mla_es)
                            wukv = sb("wukv", [128, 1024], BF16, mla_es)

                            with phase() as st:
                                tools = make_norm_tools(st)
                                load("pool", "wuq", wuq[:], WUQ[l], ["wuq"])
                                load("pool", "wukv", wukv[:], WUKV[l], ["wukv"])
                                wl = [sb("wl%d" % i, [128, 8, 128], BF16, st) for i in range(4)]
                                for i, ch in enumerate([CH_CQ, CH_CQ + 1, CH_CKV, CH_KR]):
                                    load("pool", "wl%d" % i, wl[i][:], WIN[l, ch], ["wl%d" % i])
                                ft = Rot("ft", 3, [128, BLK], F32, st)
                                for b in range(NBLK):
                                    cs = slice(b * BLK, (b + 1) * BLK)
                                    for m in range(2):
                                        mm(P[m][:], [(wl[m][:, k, :], hT[:, k, cs]) for k in range(8)], hh_(b) + ["wl%d" % m], [PH[m]])
                                    for m in range(2):
                                        t, h = tools["sq"].get()
                                        act(t[:], P[m][:], AF.Square, [PH[m]], [h])
                                        mm(P[2][:], [(ones[:], t[:])], [h, "ones"], [PH[2]], start=(m == 0), stop=(m == 1))
                                    rs, rh = rstd(tools, P[2][:], PH[2], 256, 0, 128)
                                    for m in range(2):
                                        stt("dve", cqn[:, m, cs], P[m][:], vcol(l, V_GQL + m), rs[:], ALU.mult, ALU.mult,
                                            [PH[m], rh, "vec"], ["cqn%d_%d" % (m, b)])
                                    mm(P[3][:], [(wl[2][:, k, :], hT[:, k, cs]) for k in range(8)], hh_(b) + ["wl2"], [PH[3]])
                                    t, h = tools["sq"].get()
                                    act(t[:], P[3][:], AF.Square, [PH[3]], [h])
                                    mm(P[2][:], [(ones[:], t[:])], [h, "ones"], [PH[2]])
                                    rs, rh = rstd(tools, P[2][:], PH[2], 128, 0, 128)
                                    stt("dve", ckvn[:, cs], P[3][:], vcol(l, V_GKVL), rs[:], ALU.mult, ALU.mult,
                                        [PH[3], rh, "vec"], ["ckvn_%d" % b])
                                    R = slice(64, 96)
                                    mm(P[4][R, :], [(wl[3][:, k, 64:96], hT[:, k, cs]) for k in range(8)], hh_(b) + ["wl3"], [PH[4]])
                                    mm(P[5][R, :], [(wl[3][:, k, 96:128], hT[:, k, cs]) for k in range(8)], hh_(b) + ["wl3"], [PH[5]])
                                    act(sqkr[R, cs], P[4][R, :], AF.Square, [PH[4]], ["sqkr_%d" % b])
                                    f1, f1h = ft.get()
                                    f2, f2h = ft.get()
                                    stt("dve", f1[R, :], P[4][R, :], vcol(l, V_GMK, 64, 96), cosT[R, cs], ALU.mult, ALU.mult,
                                        [PH[4], "vec", "cosT"], [f1h])
                                    stt("dve", f2[R, :], P[5][R, :], vcol(l, V_GMKS, 64, 96), sinT[R, cs], ALU.mult, ALU.mult,
                                        [PH[5], "vec", "sinT"], [f2h])
                                    tt("pool", krr[R, cs], f1[R, :], f2[R, :], ALU.add, [f1h, f2h], ["krr_%d" % b])
                                if l == 0 and s == 0:
                                    dump("cqn", cqn[:, :, :], [128, 2, S], BF16, ["cqn%d_%d" % (m, b) for m in range(2) for b in range(NBLK)])
                                    dump("ckvn", ckvn[:, :], [128, S], BF16, ["ckvn_%d" % b for b in range(NBLK)])
                                    dump("krr", krr[64:96, :], [32, S], BF16, ["krr_%d" % b for b in range(NBLK)])
                            if stop_after == "lat":
                                return True
                            with phase() as st:
                                tools = make_norm_tools(st)
                                ft = Rot("ft", 2, [128, BLK], F32, st)
                                QT = Rot("QT", 1, [128, S], BF16, st)
                                KT = Rot("KT", 1, [128, S], BF16, st)
                                VA = Rot("VA", 1, [128, NT, 128], BF16, st)
                                for i in range(1):
                                    ms("pool", VA.t[i][:, :, 64:128], 1.0, [VA.h[i] + "_ones"])
                                PTr = Rot("PTr", 3, [128, BLK], BF16, st)
                                fz = Rot("fz", 2, [128, BLK], F32, st)
                                sm_scale = 96.0 ** -0.5
                                for h in range(8):
                                    qt, qh = QT.get()
                                    kt_, kh = KT.get()
                                    va, vh = VA.get()
                                    for g4 in range(4):
                                        for i in range(4):
                                            t_ = g4 * 4 + i
                                            mm(P[3][:, i * 64:(i + 1) * 64],
                                               [(ckvn[:, t_ * 128:(t_ + 1) * 128], wukv[:, 512 + h * 64:512 + (h + 1) * 64])],
                                               ["ckvn_%d" % g4, "wukv"], [PH[3]])
                                        cp("dve", va[:, g4 * 4:(g4 + 1) * 4, 0:64], P[3][:, 0:256].rearrange("p (i d) -> p i d", d=64),
                                           [PH[3]], ["%s_%d" % (vh, g4)])
                                    for b in range(NBLK):
                                        cs = slice(b * BLK, (b + 1) * BLK)
                                        R = slice(64, 96)
                                        mm(P[0][0:64, :], [(wukv[:, h * 64:(h + 1) * 64], ckvn[:, cs])], ["wukv", "ckvn_%d" % b], [PH[0]])
                                        t, hs = tools["sq"].get()
                                        act(t[0:64, :], P[0][0:64, :], AF.Square, [PH[0]], [hs])
                                        mm(P[1][0:96, :], [(ones[0:64, 0:96], t[0:64, :]), (ones[64:96, 0:96], sqkr[R, cs])],
                                           [hs, "sqkr_%d" % b, "ones"], [PH[1]])
                                        rs, rh = rstd(tools, P[1][0:96, :], PH[1], 96, 0, 96)
                                        stt("dve", kt_[0:64, cs], P[0][0:64, :], vcol(l, V_GMK, 0, 64), rs[0:64, :], ALU.mult, ALU.mult,
                                            [PH[0], rh, "vec"], ["%s_%d" % (kh, b)])
                                        tt("pool", kt_[R, cs], krr[R, cs], rs[R, :], ALU.mult, ["krr_%d" % b, rh], ["%s_%dr" % (kh, b)])
                                        mm(P[2][0:96, :], [(wuq[:, k, h, 0:96], cqn[:, k, cs]) for k in range(2)],
                                           ["wuq", "cqn0_%d" % b, "cqn1_%d" % b], [PH[2]])
                                        mm(P[3][R, :], [(wuq[:, k, h, 96:128], cqn[:, k, cs]) for k in range(2)],
                                           ["wuq", "cqn0_%d" % b, "cqn1_%d" % b], [PH[3]])
                                        t, hs = tools["sq"].get()
                                        act(t[0:96, :], P[2][0:96, :], AF.Square, [PH[2]], [hs])
                                        mm(P[1][0:96, :], [(ones[0:96, 0:96], t[0:96, :])], [hs, "ones"], [PH[1]])
                                        rs, rh = rstd(tools, P[1][0:96, :], PH[1], 96, 0, 96)
                                        stt("dve", qt[0:64, cs], P[2][0:64, :], vcol(l, V_GMQ, 0, 64), rs[0:64, :], ALU.mult, ALU.mult,
                                            [PH[2], rh, "vec"], ["%s_%d" % (qh, b)])
                                        f1, f1h = ft.get()
                                        f2, f2h = ft.get()
                                        stt("dve", f1[R, :], P[2][R, :], vcol(l, V_GMQ, 64, 96), cosT[R, cs], ALU.mult, ALU.mult,
                                            [PH[2], "vec", "cosT"], [f1h])
                                        stt("dve", f2[R, :], P[3][R, :], vcol(l, V_GMQS, 64, 96), sinT[R, cs], ALU.mult, ALU.mult,
                                            [PH[3], "vec", "sinT"], [f2h])
                                        tt("pool", f1[R, :], f1[R, :], f2[R, :], ALU.add, [f1h, f2h], [f1h])
                                        tt("pool", qt[R, cs], f1[R, :], rs[R, :], ALU.mult, [f1h, rh], ["%s_%dr" % (qh, b)])
                                    if l == 0 and s == 0 and h == 1:
                                        dump("QT1", qt[0:96, :], [96, S], BF16, ["%s_%d" % (qh, b) for b in range(NBLK)] + ["%s_%dr" % (qh, b) for b in range(NBLK)])
                                        dump("KT1", kt_[0:96, :], [96, S], BF16, ["%s_%d" % (kh, b) for b in range(NBLK)] + ["%s_%dr" % (kh, b) for b in range(NBLK)])
                                        dump("VA1", va[:, :, :], [128, NT, 128], BF16, ["%s_%d" % (vh, g4) for g4 in range(4)] + [vh + "_ones"])
                                    for qb in range(NBLK):
                                        qcs = slice(qb * BLK, (qb + 1) * BLK)
                                        O = P[6 + qb % 2]
                                        OH = PH[6 + qb % 2]
                                        for kt in range(NT):
                                            Sb = P[4 + kt % 2]
                                            SH = PH[4 + kt % 2]
                                            kb = kt // 4
                                            mm(Sb[:], [(kt_[0:96, kt * 128:(kt + 1) * 128], qt[0:96, qcs])],
                                               ["%s_%d" % (kh, kb), "%s_%dr" % (kh, kb), "%s_%d" % (qh, qb), "%s_%dr" % (qh, qb)], [SH])
                                            pt, ph = PTr.get()
                                            act(pt[:], Sb[:], AF.Exp, [SH], [ph], scale=sm_scale)
                                            mm(O[:], [(va[:, kt, :], pt[:])], ["%s_%d" % (vh, kb), vh + "_ones", ph], [OH],
                                               start=(kt == 0), stop=(kt == NT - 1))
                                        z1, z1h = fz.get()
                                        z2, z2h = fz.get()
                                        act(z1[0:64, :], O[64:128, :], AF.Ln, [OH], [z1h])
                                        act(z2[0:64, :], z1[0:64, :], AF.Exp, [z1h], [z2h], scale=-1.0)
                                        po = 64 * (h % 2)
                                        tt("dve", oT[po:po + 64, h // 2, qcs], O[0:64, :], z2[0:64, :], ALU.mult, [OH, z2h],
                                           ["oT_%d_%d" % (h, qb)])
                                if l == 0 and s == 0:
                                    dump("oT_mla", oT[:, :, :], [128, 4, S], BF16, ["oT_%d_%d" % (h, b) for h in range(8) for b in range(NBLK)])
                            mla_es.close()
                            if stop_after == "mla":
                                return True
                            merge(l, 0, 8, oT)
                            if stop_after == "merge0":
                                return True

                            diff_es = ExitStack()
                            stacks.append(diff_es)
                            diff_es.__enter__()
                            VD = sb("VD", [128, NT, 512], BF16, diff_es)
                            with phase() as st:
                                wdv = sb("wdv", [128, 8, 512], BF16, st)
                                for h in range(4):
                                    load("pool", "wdv%d" % h, wdv[:, :, h * 128:(h + 1) * 128], WIN[l, CH_DV + h], ["wdv%d" % h])
                                for t_ in range(NT):
                                    mm(P[t_ % 2][:], [(hT[:, k, t_ * 128:(t_ + 1) * 128], wdv[:, k, :]) for k in range(8)],
                                       hh_(t_ // 4) + ["wdv%d" % h for h in range(4)], [PH[t_ % 2]])
                                    act(VD[:, t_, :], P[t_ % 2][:], AF.Copy, [PH[t_ % 2]], ["VD_%d_%d" % (t_ // 4, t_ % 4)])
                            with phase() as st:
                                tools = make_norm_tools(st)
                                wrot = Rot("wch", 2, [128, 8, 128], BF16, st)
                                QT = Rot("QT", 1, [128, S], BF16, st)
                                KT = Rot("KT", 1, [128, S], BF16, st)
                                dD = Rot("dD", 2, [128, BLK], F32, st)
                                tb = Rot("tb", 3, [128, BLK], F32, st)
                                PTr = Rot("PTr", 3, [128, BLK], BF16, st)
                                fz = Rot("fz", 4, [128, BLK], F32, st)
                                od = sb("od", [128, BLK], F32, st)
                                sm_scale = 64.0 ** -0.5
                                for h in range(4):
                                    slope = 2.0 ** (-8.0 * (9 + h) / 12.0)
                                    cfac = slope / sm_scale
                                    qt, qh = QT.get()
                                    kt_, kh = KT.get()
                                    for (ch, dst, dh, gcol) in ((CH_DQ + h, qt, qh, V_GDQ), (CH_DK + h, kt_, kh, V_GDK)):
                                        w, wh = wrot.get()
                                        load("pool", wh, w[:], WIN[l, ch], [wh])
                                        for b in range(NBLK):
                                            cs = slice(b * BLK, (b + 1) * BLK)
                                            mm(P[0][:], [(w[:, k, :], hT[:, k, cs]) for k in range(8)], hh_(b) + [wh], [PH[0]])
                                            t, hs = tools["sq"].get()
                                            act(t[:], P[0][:], AF.Square, [PH[0]], [hs])
                                            mm(P[1][:], [(bd[:], t[:])], [hs, "bd"], [PH[1]])
                                            rs, rh = rstd(tools, P[1][:], PH[1], 64, 0, 128)
                                            stt("dve", dst[:, cs], P[0][:], vcol(l, gcol), rs[:], ALU.mult, ALU.mult,
                                                [PH[0], rh, "vec"], ["%s_%d" % (dh, b)])
                                    if l == 0 and s == 0 and h == 1:
                                        dump("dQT1", qt[:, :], [128, S], BF16, ["%s_%d" % (qh, b) for b in range(NBLK)])
                                        dump("dKT1", kt_[:, :], [128, S], BF16, ["%s_%d" % (kh, b) for b in range(NBLK)])
                                    for qb in range(NBLK):
                                        qcs = slice(qb * BLK, (qb + 1) * BLK)
                                        for kt in range(NT):
                                            kb = kt // 4
                                            d_, ddh = dD.get()
                                            act(d_[:], posq[:, qcs], AF.Abs, ["posq", "nkc"], [ddh], scale=cfac, bias=nkc[:, kt, 8 + h:9 + h])
                                            for m in range(2):
                                                pr = slice(64 * m, 64 * m + 64)
                                                mm(P[2 + m][:], [(kt_[pr, kt * 128:(kt + 1) * 128], qt[pr, qcs])],
                                                   ["%s_%d" % (kh, kb), "%s_%d" % (qh, qb)], [PH[2 + m]])
                                                t_b, tbh = tb.get()
                                                stt("dve", t_b[:], d_[:], 0.0, P[2 + m][:], ALU.max, ALU.subtract, [ddh, PH[2 + m]], [tbh])
                                                pt, ph = PTr.get()
                                                act(pt[:], t_b[:], AF.Exp, [tbh], [ph], scale=-sm_scale)
                                                mm(P[4 + m][:], [(VD[:, kt, h * 128:(h + 1) * 128], pt[:])], ["VD_%d_%d" % (kb, kt % 4), ph], [PH[4 + m]],
                                                   start=(kt == 0), stop=(kt == NT - 1))
                                                mm(P[6 + m][:], [(ones[:], pt[:])], ["ones", ph], [PH[6 + m]],
                                                   start=(kt == 0), stop=(kt == NT - 1))
                                        rz = []
                                        for m in range(2):
                                            z1, z1h = fz.get()
                                            z2, z2h = fz.get()
                                            act(z1[:], P[6 + m][:], AF.Ln, [PH[6 + m]], [z1h])
                                            act(z2[:], z1[:], AF.Exp, [z1h], [z2h], scale=-1.0)
                                            rz.append((z2, z2h))
                                        t1, t1h = tb.get()
                                        t2, t2h = tb.get()
                                        tt("dve", t1[:], P[4][:], rz[0][0][:], ALU.mult, [PH[4], rz[0][1]], [t1h])
                                        stt("dve", t2[:], P[5][:], lamt[:, l, 0:1], rz[1][0][:], ALU.mult, ALU.mult, [PH[5], rz[1][1], "lam"], [t2h])
                                        tt("pool", od[:], t1[:], t2[:], ALU.subtract, [t1h, t2h], ["od"])
                                        t, hs = tools["sq"].get()
                                        act(t[:], od[:], AF.Square, ["od"], [hs])
                                        mm(P[0][:], [(ones[:], t[:])], [hs, "ones"], [PH[0]])
                                        rs, rh = rstd(tools, P[0][:], PH[0], 128, 0, 128, layer_bias=l)
                                        stt("dve", oT[:, h, qcs], od[:], vcol(l, V_GDO), rs[:], ALU.mult, ALU.mult, ["od", rh, "vec"],
                                            ["oT_%d_%d" % (h, qb)])
                                if l == 0 and s == 0:
                                    dump("oT_diff", oT[:, :, :], [128, 4, S], BF16, ["oT_%d_%d" % (h, b) for h in range(4) for b in range(NBLK)])
                            diff_es.close()
                            if stop_after == "diff":
                                return True
                            merge(l, 1, 4, oT)
                            if stop_after == "merge1":
                                return True

                            with phase() as st:
                                tools = make_norm_tools(st)
                                nmk = sb("nmk", [128, 6, BLK], BF16, st)
                                load("pool", "nmk", nmk[:], NM[:, :, :], ["nmk"])
                                wsv = sb("wsv", [128, 8, 128], BF16, st)
                                load("pool", "wsv", wsv[:], WIN[l, CH_SV], ["wsv"])
                                VW = sb("VW", [128, NT, 2, 128], BF16, st)
                                ms("pool", VW[:, :, :, 64:128], 1.0, ["VW_ones"])
                                for t_ in range(NT):
                                    mm(P[t_ % 2][:, 0:128], [(hT[:, k, t_ * 128:(t_ + 1) * 128], wsv[:, k, :]) for k in range(8)],
                                       hh_(t_ // 4) + ["wsv"], [PH[t_ % 2]])
                                    cp("dve", VW[:, t_, :, 0:64], P[t_ % 2][:, 0:128].rearrange("p (g d) -> p g d", g=2),
                                       [PH[t_ % 2]], ["VW_%d_%d" % (t_ // 4, t_ % 4)])
                                wrot = Rot("wch", 2, [128, 8, 128], BF16, st)
                                KTw = sb("KTw", [128, S], BF16, st)
                                QTw = sb("QTw", [128, 2, S], BF16, st)
                                dD = Rot("dD", 3, [128, BLK], F32, st)
                                tb = Rot("tb", 3, [128, BLK], F32, st)
                                PTr = Rot("PTr", 3, [128, BLK], BF16, st)
                                fz = Rot("fz", 2, [128, BLK], F32, st)
                                sm_scale = 64.0 ** -0.5
                                for g in range(2):
                                    jobs = [(CH_SK + g, KTw, None, "KTw", V_GWK)] + [(CH_SQ + 2 * g + c, QTw, c, "QTw%d" % c, V_GWQ) for c in range(2)]
                                    for (ch, dst, ci, dh, gcol) in jobs:
                                        w, wh = wrot.get()
                                        load("pool", wh, w[:], WIN[l, ch], [wh])
                                        for b in range(NBLK):
                                            cs = slice(b * BLK, (b + 1) * BLK)
                                            mm(P[0][:], [(w[:, k, :], hT[:, k, cs]) for k in range(8)], hh_(b) + [wh], [PH[0]])
                                            t, hs = tools["sq"].get()
                                            act(t[:], P[0][:], AF.Square, [PH[0]], [hs])
                                            mm(P[1][:], [(bd[:], t[:])], [hs, "bd"], [PH[1]])
                                            rs, rh = rstd(tools, P[1][:], PH[1], 64, 0, 128)
                                            dap = dst[:, cs] if ci is None else dst[:, ci, cs]
                                            stt("dve", dap, P[0][:], vcol(l, gcol), rs[:], ALU.mult, ALU.mult,
                                                [PH[0], rh, "vec"], ["%s_%d" % (dh, b)])
                                    for qb in range(NBLK):
                                        qcs = slice(qb * BLK, (qb + 1) * BLK)
                                        kts = list(range(max(0, 4 * qb - 1), min(NT, 4 * qb + 5)))
                                        for kt in kts:
                                            kb = kt // 4
                                            r_i = kt - 4 * qb + 1
                                            for i in range(4):
                                                hh = 4 * g + i
                                                c = i // 2
                                                pr = slice(64 * (i % 2), 64 * (i % 2) + 64)
                                                slope = 2.0 ** (-8.0 * (hh + 1) / 12.0)
                                                d_, ddh = dD.get()
                                                act(d_[:], posq[:, qcs], AF.Abs, ["posq", "nkc"], [ddh], scale=slope / sm_scale, bias=nkc[:, kt, hh:hh + 1])
                                                Sb = P[2 + i % 2]
                                                SH = PH[2 + i % 2]
                                                mm(Sb[:], [(KTw[pr, kt * 128:(kt + 1) * 128], QTw[pr, c, qcs]), (ident[:], nmk[:, r_i, :])],
                                                   ["KTw_%d" % kb, "QTw%d_%d" % (c, qb), "ident", "nmk"], [SH])
                                                t_b, tbh = tb.get()
                                                stt("dve", t_b[:], d_[:], 0.0, Sb[:], ALU.max, ALU.subtract, [ddh, SH], [tbh])
                                                pt, ph = PTr.get()
                                                act(pt[:], t_b[:], AF.Exp, [tbh], [ph], scale=-sm_scale)
                                                mm(P[4 + i][:], [(VW[:, kt, g, :], pt[:])], ["VW_%d_%d" % (kb, kt % 4), "VW_ones", ph], [PH[4 + i]],
                                                   start=(kt == kts[0]), stop=(kt == kts[-1]))
                                        for i in range(4):
                                            hh = 4 * g + i
                                            z1, z1h = fz.get()
                                            z2, z2h = fz.get()
                                            sc.op("act", lambda e, z1=z1, i=i, hh=hh: e.activation(out=z1[0:64, :], in_=P[4 + i][64:128, :], func=AF.Ln,
                                                                                         bias=esink[0:64, l, hh:hh + 1]),
                                                  [PH[4 + i], "esink"], [z1h])
                                            act(z2[0:64, :], z1[0:64, :], AF.Exp, [z1h], [z2h], scale=-1.0)
                                            po = 64 * (hh % 2)
                                            tt("dve", oT[po:po + 64, hh // 2, qcs], P[4 + i][0:64, :], z2[0:64, :], ALU.mult, [PH[4 + i], z2h],
                                               ["oT_%d_%d" % (hh, qb)])
                                if l == 0 and s == 0:
                                    dump("oT_win", oT[:, :, :], [128, 4, S], BF16, ["oT_%d_%d" % (h, b) for h in range(8) for b in range(NBLK)])
                            if stop_after == "win":
                                return True
                            merge(l, 2, 8, oT)
                            if l == 0 and s == 0:
                                dump("x_mix", xT[:, :, :], [128, 8, S], F32, sum([xh(b) for b in range(NBLK)], []))
                            mix_es.close()
                            if stop_after == "merge2":
                                return True

                            with phase() as st:
                                tools = make_norm_tools(st)
                                for b in range(NBLK):
                                    norm_block(tools, l, V_GFFN, b, hT, slice(b * BLK, (b + 1) * BLK), lambda k, b: "hT%d_%d" % (k, b))
                            with phase() as st:
                                u = sb("u", [128, NFC, 2 * BLK], BF16, st)
                                abuf = Rot("ab", 2, [128, BLK + 2], F32, st)
                                cvr = Rot("cv", 2, [128, BLK], F32, st)
                                ger = Rot("ge", 2, [128, BLK], F32, st)
                                wgr = Rot("wg", 2, [128, 8, 128], BF16, st)
                                wur = Rot("wu", 2, [128, 8, 128], BF16, st)
                                wdr = Rot("wd", 2, [128, NFC, 128], BF16, st)
                                for sbk in range(2):
                                    for c in range(NFC):
                                        wg, wgh = wgr.get()
                                        wu, wuh = wur.get()
                                        load("pool", wgh, wg[:], WG[l, c], [wgh])
                                        load("pool", wuh, wu[:], WU[l, c], [wuh])
                                        for sub in range(2):
                                            b = sbk * 2 + sub
                                            t0 = b * BLK
                                            cs = slice(t0, t0 + BLK)
                                            mm(P[sub][:], [(wg[:, k, :], hT[:, k, cs]) for k in range(8)], hh_(b) + [wgh], [PH[sub]])
                                            mm(P[2 + sub][:], [(wu[:, k, :], hT[:, k, cs]) for k in range(8)], hh_(b) + [wuh], [PH[2 + sub]])
                                            ab, abh = abuf.get()
                                            has_lo = t0 > 0
                                            has_hi = t0 + BLK < S
                                            if has_lo and has_hi:
                                                c0 = t0 - 1
                                                hs_ = slice(c0, c0 + BLK + 2, BLK + 1)
                                                lo_col, hi_col = 0, 1
                                                nb = [b - 1, b + 1]
                                            elif has_hi:
                                                hs_ = slice(t0 + BLK, t0 + BLK + 2)
                                                lo_col, hi_col = None, 0
                                                nb = [b + 1]
                                            else:
                                                hs_ = slice(t0 - 2, t0)
                                                lo_col, hi_col = 1, None
                                                nb = [b - 1]
                                            mm(P[4 + sub][:, 0:2], [(wg[:, k, :], hT[:, k, hs_]) for k in range(8)],
                                               sum([hh_(x) for x in nb], []) + [wgh], [PH[4 + sub]])
                                            act(ab[:, 1:BLK + 1], P[sub][:], AF.Copy, [PH[sub]], [abh + "m"])
                                            if lo_col is None:
                                                ms("pool", ab[:, 0:1], 0.0, [abh + "l"])
                                            else:
                                                cp("dve", ab[:, 0:1], P[4 + sub][:, lo_col:lo_col + 1], [PH[4 + sub]], [abh + "l"])
                                            if hi_col is None:
                                                ms("pool", ab[:, BLK + 1:BLK + 2], 0.0, [abh + "h"])
                                            else:
                                                cp("dve", ab[:, BLK + 1:BLK + 2], P[4 + sub][:, hi_col:hi_col + 1], [PH[4 + sub]], [abh + "h"])
                                            cv, cvh = cvr.get()
                                            abr = [abh + "m", abh + "l", abh + "h"]
                                            ts("pool", cv[:], ab[:, 0:BLK], vcol(l, V_CW + c * 3 + 0), vcol(l, V_CB + c), ALU.mult, ALU.add,
                                               abr + ["vec"], [cvh])
                                            stt("dve", cv[:], ab[:, 1:BLK + 1], vcol(l, V_CW + c * 3 + 1), cv[:], ALU.mult, ALU.add, abr + [cvh, "vec"], [cvh])
                                            stt("dve", cv[:], ab[:, 2:BLK + 2], vcol(l, V_CW + c * 3 + 2), cv[:], ALU.mult, ALU.add, abr + [cvh, "vec"], [cvh])
                                            ge, geh = ger.get()
                                            act(ge[:], cv[:], AF.Gelu, [cvh], [geh])
                                            tt("dve", u[:, c, sub * BLK:(sub + 1) * BLK], ge[:], P[2 + sub][:], ALU.mult, [geh, PH[2 + sub]],
                                               ["u%d_%d" % (c, sub)])
                                    for j in range(8):
                                        wd, wdh = wdr.get()
                                        load("pool", wdh, wd[:], WD[l, j], [wdh])
                                        for sub in range(2):
                                            b = sbk * 2 + sub
                                            cs = slice(b * BLK, (b + 1) * BLK)
                                            mm(P[6 + sub][:], [(wd[:, c, :], u[:, c, sub * BLK:(sub + 1) * BLK]) for c in range(NFC)],
                                               ["u%d_%d" % (c, sub) for c in range(NFC)] + [wdh], [PH[6 + sub]])
                                            tt("dve", xT[:, j, cs], P[6 + sub][:], xT[:, j, cs], ALU.add, [PH[6 + sub], "xT%d_%d" % (j, b)],
                                               ["xT%d_%d" % (j, b)])
                                if l == 0 and s == 0:
                                    dump("x_ffn", xT[:, :, :], [128, 8, S], F32, sum([xh(b) for b in range(NBLK)], []))
                            if stop_after == "ffn":
                                return True

                            with phase() as st:
                                tools = make_norm_tools(st)
                                pT = sb("pT", [128, 2, S], BF16, st)
                                load("pool", "pT", pT[:], PT[l, s], ["pT"])
                                h3 = sb("h3", [128, 8, BLK], BF16, st)
                                Eb = sb("Eb", [128, 8, BLK], F32, st)
                                wpr = Rot("wpp", 2, [128, 2, 128], BF16, st)
                                wgr = Rot("wpg", 3, [128, 8, 128], BF16, st)
                                sg = Rot("sg", 2, [128, BLK], F32, st)
                                ee = Rot("ee", 2, [128, BLK], F32, st)
                                last = (l == n_layers - 1)
                                for b in range(NBLK):
                                    cs = slice(b * BLK, (b + 1) * BLK)
                                    norm_block(tools, l, V_GPLEIN, b, h3, slice(0, BLK), lambda k, b: "h3_%d" % k)
                                    for j in range(8):
                                        w, wh = wpr.get()
                                        load("pool", wh, w[:], WPP[l, j], [wh])
                                        pe_ = P[1 + j % 2]
                                        peh = PH[1 + j % 2]
                                        mm(pe_[:], [(w[:, k, :], pT[:, k, cs]) for k in range(2)], ["pT", wh], [peh])
                                        act(Eb[:, j, :], pe_[:], AF.Copy, [peh], ["Eb%d" % j])
                                        t, hs = tools["sq"].get()
                                        act(t[:], pe_[:], AF.Square, [peh], [hs])
                                        mm(P[3][:], [(ones[:], t[:])], [hs, "ones"], [PH[3]], start=(j == 0), stop=(j == 7))
                                    rse, rseh = rstd(tools, P[3][:], PH[3], DM, 0, 128)
                                    for j in range(8):
                                        w, wh = wgr.get()
                                        load("pool", wh, w[:], WPG[l, j], [wh])
                                        G = P[4 + j % 2]
                                        GH = PH[4 + j % 2]
                                        mm(G[:], [(w[:, k, :], h3[:, k, :]) for k in range(8)], ["h3_%d" % k for k in range(8)] + [wh], [GH])
                                        s_, sh = sg.get()
                                        act(s_[:], G[:], AF.Tanh, [GH], [sh], scale=0.5)
                                        e_, eh = ee.get()
                                        stt("dve", e_[:], Eb[:, j, :], vcol(l, V_GPLE + j), rse[:], ALU.mult, ALU.mult, ["Eb%d" % j, rseh, "vec"], [eh])
                                        stt("dve", e_[:], s_[:], 1.0, e_[:], ALU.add, ALU.mult, [sh, eh], [eh])
                                        stt("dve", xT[:, j, cs], e_[:], 0.5, xT[:, j, cs], ALU.mult, ALU.add, [eh, "xT%d_%d" % (j, b)],
                                            ["xT%d_%d" % (j, b)])
                                    if last:
                                        load("sp", "out%d" % b, YT[s, :, :, cs], xT[:, :, cs], ["yt_%d_%d" % (s, b)], xh(b))
                            if stop_after == "ple":
                                return True
                        finally:
                            for st_ in reversed(stacks):
                                st_.close()
                        return False

                    if not stopped:
                        for l in range(n_layers):
                            if run_layer(l):
                                stopped = True
                                break
                if stopped:
                    break
        if stopped:
            for b in range(NBLK):
                cs = slice(b * BLK, (b + 1) * BLK)
                load("sp", "out%d" % b, YT[0, :, :, cs], xT[:, :, cs], ["yt_stop_%d" % b], xh(b))
        sc.barrier()
        sc.flush()
        build_program.n_ins = sc.n_ins
    return nc, dbg_names


_CACHE = {}


LAUNCH_CORES = 2


def kernel(**inputs):
    shared = prep_shared(inputs)
    if "nc" not in _CACHE:
        _CACHE["nc"] = build_program()[0]
    nc = _CACHE["nc"]
    out = np.empty((N_CORES * SEQ_PER_CORE, S, DM), np.float32)
    for g0 in range(0, N_CORES, LAUNCH_CORES):
        in_maps = []
        for c in range(g0, g0 + LAUNCH_CORES):
            m = dict(shared)
            m.update(prep_core(inputs, c))
            in_maps.append(m)
        res = run_bass_kernel_spmd(nc, in_maps, core_ids=list(range(LAUNCH_CORES)))
        for i, c in enumerate(range(g0, g0 + LAUNCH_CORES)):
            yt = np.asarray(res.results[i]["YT"])
            out[c * SEQ_PER_CORE:(c + 1) * SEQ_PER_CORE] = yt.transpose(0, 3, 2, 1).reshape(SEQ_PER_CORE, S, DM)
    return out
```
